# Optimizing a Trainium2 kernel written in Bass

```python
import math
import jax, jax.numpy as jnp
from jax import lax
import numpy as np

D_MODEL = 2048
BATCH = 4
SEQ = 8192
DEPTH = 2

N_A_LAYERS = max(1, DEPTH // 2)
N_B_LAYERS = DEPTH - N_A_LAYERS

SB_HEAD_DIM = 128
SB_HEADS = D_MODEL // SB_HEAD_DIM

DIFF_HEAD_DIM = 128
DIFF_HEADS = D_MODEL // (2 * DIFF_HEAD_DIM)
DIFF_V_DIM = 2 * DIFF_HEAD_DIM
DIFF_QK_WIDTH = 2 * DIFF_HEADS * DIFF_HEAD_DIM
DIFF_V_WIDTH = DIFF_HEADS * DIFF_V_DIM

ROT_DIM = DIFF_HEAD_DIM // 4
ROPE_THETA = 500000.0

D_FF = 4 * D_MODEL

DEEPNORM_ALPHA = (2 * DEPTH) ** 0.25
DEEPNORM_BETA = (8 * DEPTH) ** -0.25

BLOCK_Q = 128
LN_EPS = 1e-5
SUBLN_EPS = 1e-5

kernel_name = "yoco_stickbreak_diffattn_hybrid"


def lambda_init_for(layer):
    return 0.8 - 0.6 * math.exp(-0.3 * layer)


def layer_norm(x, g, b):
    xf = x.astype(jnp.float32)
    mu = jnp.mean(xf, axis=-1, keepdims=True)
    var = jnp.mean(jnp.square(xf - mu), axis=-1, keepdims=True)
    y = (xf - mu) * lax.rsqrt(var + LN_EPS) * g.astype(jnp.float32) + b.astype(jnp.float32)
    return y.astype(x.dtype)


def squared_relu_mlp(x, w_up, w_down):
    h = jax.nn.relu(x @ w_up)
    return (h * h) @ w_down


def rotary_tables(seq_len):
    inv_freq = ROPE_THETA ** (-jnp.arange(0, ROT_DIM, 2, dtype=jnp.float32) / ROT_DIM)
    ang = jnp.arange(seq_len, dtype=jnp.float32)[:, None] * inv_freq[None, :]
    return jnp.cos(ang), jnp.sin(ang)


def partial_rotary(t, cos, sin):
    half = ROT_DIM // 2
    r1 = t[..., :half].astype(jnp.float32)
    r2 = t[..., half:ROT_DIM].astype(jnp.float32)
    rotated = jnp.concatenate([r1 * cos - r2 * sin, r2 * cos + r1 * sin], axis=-1).astype(t.dtype)
    return jnp.concatenate([rotated, t[..., ROT_DIM:]], axis=-1)


def merge_blocks(o):
    n_blk, b, h, bq, dv = o.shape
    return o.transpose(1, 0, 3, 2, 4).reshape(b, n_blk * bq, h * dv)


def stick_breaking_attention(x, w_qkv, w_o):
    b, s, _ = x.shape
    n_blk = s // BLOCK_Q
    qkv = (x @ w_qkv).reshape(b, s, 3, SB_HEADS, SB_HEAD_DIM).transpose(2, 0, 3, 1, 4)
    q, k, v = qkv[0], qkv[1], qkv[2]
    q_blocks = q.reshape(b, SB_HEADS, n_blk, BLOCK_Q, SB_HEAD_DIM).transpose(2, 0, 1, 3, 4)
    scale = SB_HEAD_DIM ** -0.5
    k_pos = jnp.arange(s)

    def block(args):
        qb, blk = args
        q_pos = blk * BLOCK_Q + jnp.arange(BLOCK_Q)
        z = jnp.einsum('bhqd,bhkd->bhqk', qb, k, preferred_element_type=jnp.float32) * scale
        strict = k_pos[None, :] < q_pos[:, None]
        neg_log_keep = jnp.where(strict, jax.nn.softplus(z), 0.0)
        shifted = jnp.concatenate([neg_log_keep[..., 1:], jnp.zeros_like(neg_log_keep[..., :1])], axis=-1)
        later = lax.cumsum(shifted, axis=3, reverse=True)
        log_a = jax.nn.log_sigmoid(z) - later
        a = jnp.where(strict, jnp.exp(log_a), 0.0)
        return jnp.einsum('bhqk,bhkd->bhqd', a.astype(v.dtype), v)

    o = lax.map(block, (q_blocks, jnp.arange(n_blk)))
    return merge_blocks(o) @ w_o


def shared_kv(x, w_kv):
    b, s, _ = x.shape
    kv = x @ w_kv
    k = kv[..., :DIFF_QK_WIDTH].reshape(b, s, DIFF_HEADS, 2, DIFF_HEAD_DIM).transpose(0, 2, 3, 1, 4)
    v = kv[..., DIFF_QK_WIDTH:].reshape(b, s, DIFF_HEADS, DIFF_V_DIM).transpose(0, 2, 1, 3)
    cos, sin = rotary_tables(s)
    return partial_rotary(k, cos, sin), v


def differential_attention(x, k, v, w_q, lam, subln_g, w_o, lambda_init):
    b, s, _ = x.shape
    n_blk = s // BLOCK_Q
    cos, sin = rotary_tables(s)
    q = (x @ w_q).reshape(b, s, DIFF_HEADS, 2, DIFF_HEAD_DIM).transpose(0, 2, 3, 1, 4)
    q = partial_rotary(q, cos, sin)
    q_blocks = q.reshape(b, DIFF_HEADS, 2, n_blk, BLOCK_Q, DIFF_HEAD_DIM).transpose(3, 0, 1, 2, 4, 5)
    lam_f = lam.astype(jnp.float32)
    lam_full = (jnp.exp(jnp.sum(lam_f[0] * lam_f[1])) - jnp.exp(jnp.sum(lam_f[2] * lam_f[3]))
                + lambda_init)
    g = subln_g.astype(jnp.float32) * (1.0 - lambda_init)
    scale = DIFF_HEAD_DIM ** -0.5
    k_pos = jnp.arange(s)

    def block(args):
        qb, blk = args
        q_pos = blk * BLOCK_Q + jnp.arange(BLOCK_Q)
        sc = jnp.einsum('bhcqd,bhckd->bhcqk', qb, k, preferred_element_type=jnp.float32) * scale
        causal = k_pos[None, :] <= q_pos[:, None]
        p = jax.nn.softmax(jnp.where(causal, sc, -jnp.inf), axis=-1)
        w = p[:, :, 0] - lam_full * p[:, :, 1]
        o = jnp.einsum('bhqk,bhkd->bhqd', w.astype(v.dtype), v).astype(jnp.float32)
        o = o * lax.rsqrt(jnp.mean(o * o, axis=-1, keepdims=True) + SUBLN_EPS) * g
        return o.astype(x.dtype)

    o = lax.map(block, (q_blocks, jnp.arange(n_blk)))
    return merge_blocks(o) @ w_o


def setup_inputs(seed: int = 0) -> dict:
    key = jax.random.key(seed)
    ks = jax.random.split(key, 13)
    f32 = jnp.float32
    x = jax.random.normal(ks[0], (BATCH, SEQ, D_MODEL), f32)
    ln_g = 1.0 + 0.02 * jax.random.normal(ks[1], (DEPTH, 2, D_MODEL), f32)
    ln_b = 0.02 * jax.random.normal(ks[2], (DEPTH, 2, D_MODEL), f32)
    sb_w_qkv = jax.random.normal(ks[3], (N_A_LAYERS, D_MODEL, 3 * D_MODEL), f32) * D_MODEL ** -0.5
    sb_w_o = jax.random.normal(ks[4], (N_A_LAYERS, D_MODEL, D_MODEL), f32) * (D_MODEL ** -0.5 * DEEPNORM_BETA)
    kv_w = jax.random.normal(ks[5], (D_MODEL, DIFF_QK_WIDTH + DIFF_V_WIDTH), f32) * D_MODEL ** -0.5
    diff_w_q = jax.random.normal(ks[6], (N_B_LAYERS, D_MODEL, DIFF_QK_WIDTH), f32) * D_MODEL ** -0.5
    diff_lambda = 0.1 * jax.random.normal(ks[7], (N_B_LAYERS, 4, DIFF_HEAD_DIM), f32)
    diff_subln_g = 1.0 + 0.02 * jax.random.normal(ks[8], (N_B_LAYERS, DIFF_V_DIM), f32)
    diff_w_o = jax.random.normal(ks[9], (N_B_LAYERS, DIFF_V_WIDTH, D_MODEL), f32) * (DIFF_V_WIDTH ** -0.5 * DEEPNORM_BETA)
    mlp_w_up = jax.random.normal(ks[10], (DEPTH, D_MODEL, D_FF), f32) * D_MODEL ** -0.5
    mlp_w_down = jax.random.normal(ks[11], (DEPTH, D_FF, D_MODEL), f32) * (D_FF ** -0.5 * DEEPNORM_BETA)
    return {"x": x, "ln_g": ln_g, "ln_b": ln_b, "sb_w_qkv": sb_w_qkv, "sb_w_o": sb_w_o,
            "kv_w": kv_w, "diff_w_q": diff_w_q, "diff_lambda": diff_lambda,
            "diff_subln_g": diff_subln_g, "diff_w_o": diff_w_o,
            "mlp_w_up": mlp_w_up, "mlp_w_down": mlp_w_down}


def reference(x, ln_g, ln_b, sb_w_qkv, sb_w_o, kv_w, diff_w_q, diff_lambda, diff_subln_g,
              diff_w_o, mlp_w_up, mlp_w_down):
    k_shared = None
    v_shared = None
    for layer in range(DEPTH):
        if layer < N_A_LAYERS:
            h = stick_breaking_attention(x, sb_w_qkv[layer], sb_w_o[layer])
        else:
            if layer == N_A_LAYERS:
                k_shared, v_shared = shared_kv(x, kv_w)
            j = layer - N_A_LAYERS
            h = differential_attention(x, k_shared, v_shared, diff_w_q[j], diff_lambda[j],
                                       diff_subln_g[j], diff_w_o[j], lambda_init_for(layer))
        x = layer_norm(DEEPNORM_ALPHA * x + h, ln_g[layer, 0], ln_b[layer, 0])
        x = layer_norm(DEEPNORM_ALPHA * x + squared_relu_mlp(x, mlp_w_up[layer], mlp_w_down[layer]),
                       ln_g[layer, 1], ln_b[layer, 1])
    return x
```

```python
import contextlib
import math
import numpy as np
import concourse.bass as bass
import concourse.mybir as mybir
from concourse.bass_utils import run_bass_kernel_spmd

F32 = mybir.dt.float32
BF16 = mybir.dt.bfloat16
AF = mybir.ActivationFunctionType
ALU = mybir.AluOpType

D = 2048
KC = 16
DFF = 8192
HC = 64
DEPTH = 2
ALPHA = (2 * DEPTH) ** 0.25
LN_EPS = 1e-5
SUBLN_EPS = 1e-5
SCALE = 128 ** -0.5
LAMBDA_INIT = 0.8 - 0.6 * math.exp(-0.3 * 1)
GROUPS = [[0, 1], [2, 3], [4, 5], [6, 7]]
ENGS = ("sync", "scalar", "vector", "gpsimd", "tensor")


class Ctr:
    def __init__(self, h, name):
        self.h = h
        self.n = 0
        self.name = name


class Phase:
    def __init__(self, nc, name):
        self.nc = nc
        self.name = name
        self.ops = {e: [] for e in ENGS}

    def ctr(self, name):
        return Ctr(self.nc.alloc_semaphore(name=f"{self.name}_{name}"), name)

    def sb(self, name, shape, dt):
        return self.nc.alloc_sbuf_tensor(f"{self.name}_{name}", shape, dt)

    def ps(self, name):
        return self.nc.alloc_psum_tensor(f"{self.name}_{name}", [128, 512], F32)

    def add(self, eng, fn, waits=(), inc=None, amt=1):
        ev = None
        if inc is not None:
            inc.n += (1 if amt is None else amt)
            ev = (inc, inc.n)
        self.ops[eng].append((fn, [w for w in waits if w is not None], inc, amt))
        return ev

    def emit(self):
        with self.nc.Block() as block:
            for e in ENGS:
                if not self.ops[e]:
                    continue

                def body(eng, e=e):
                    seen = {}
                    for fn, waits, inc, amt in self.ops[e]:
                        for (c, v) in waits:
                            assert v <= c.n, (self.name, c.name, v, c.n)
                            assert v < 60000, (self.name, c.name, v)
                            if seen.get(id(c), 0) >= v:
                                continue
                            eng.wait_ge(c.h, v)
                            seen[id(c)] = v
                        if fn is None:
                            continue
                        ins = fn(eng)
                        if inc is not None:
                            if amt is None:
                                ins.then_inc(inc.h)
                            else:
                                ins.then_inc(inc.h, amt)

                getattr(block, e)(body)


class LoadRing:
    def __init__(self, P, name, nslots, width, dt, queue="sync"):
        self.P = P
        self.n = nslots
        self.w = width
        self.buf = P.sb(name, [128, nslots * width], dt)
        self.ld = [P.ctr(f"{name}_l{i}") for i in range(nslots)]
        self.free = [None] * nslots
        self.queue = queue
        self.srcs = []
        self.issued = 0

    def schedule(self, srcs):
        self.srcs = list(srcs)

    def _issue(self):
        k = self.issued
        if k >= len(self.srcs):
            return
        s = k % self.n
        src = self.srcs[k]
        if not isinstance(src, list):
            src = [(0, None, src)]
        evs = None
        for (off, wd, ap) in src:
            def fn(e, s=s, off=off, wd=wd, ap=ap):
                if wd is None:
                    o = self.buf[:, s * self.w:(s + 1) * self.w]
                else:
                    o = self.buf[:, s * self.w + off:s * self.w + off + wd]
                if len(ap.shape) == 3:
                    o = o.rearrange("p (a b) -> p a b", a=ap.shape[1])
                return e.dma_start(out=o, in_=ap)
            evs = self.P.add(self.queue, fn, waits=[self.free[s]], inc=self.ld[s], amt=16)
        self.ev = getattr(self, "ev", {})
        self.ev[k] = evs
        self.issued += 1

    def start(self):
        for _ in range(self.n):
            self._issue()

    def get(self, k):
        assert k < self.issued, (k, self.issued)
        return (k % self.n) * self.w, self.ev[k]

    def done(self, k, evt):
        self.free[k % self.n] = evt
        assert self.issued == k + self.n or self.issued == len(self.srcs), (k, self.issued)
        self._issue()


class PsRing:
    def __init__(self, P, name, n):
        self.t = [P.ps(f"{name}{i}") for i in range(n)]
        self.free = [None] * n
        self.k = 0
        self.n = n

    def get(self):
        i = self.k % self.n
        self.k += 1
        return i, self.t[i], self.free[i]

    def done(self, i, evt):
        self.free[i] = evt


def act(out, in_, func, **kw):
    return lambda e: e.activation(out, in_, func, **kw)


def tt(out, a, b, op):
    return lambda e: e.tensor_tensor(out, a, b, op)


def mm(out, lhsT, rhs, start, stop):
    return lambda e: e.matmul(out, lhsT, rhs, start=start, stop=stop)


def dma(out, in_):
    return lambda e: e.dma_start(out=out, in_=in_)


def phase_prep(nc, casts):
    with nc.cleanup_on_exit():
        P = Phase(nc, "prep")
        NS = 3
        stg = P.sb("stg", [128, NS * 8192], BF16)
        ld = [P.ctr(f"ld{i}") for i in range(NS)]
        st = [P.ctr(f"st{i}") for i in range(NS)]
        k = 0
        for (src, dst, rows, W) in casts:
            for r0 in range(0, rows, 128):
                nr = min(128, rows - r0)
                s = k % NS
                w = [(st[s], st[s].n)] if st[s].n else []
                ev = P.add("gpsimd", dma(stg[0:nr, s * 8192:s * 8192 + W], src[r0:r0 + nr, :]),
                           waits=w, inc=ld[s], amt=16)
                P.add("sync", dma(dst[r0:r0 + nr, :], stg[0:nr, s * 8192:s * 8192 + W]),
                      waits=[ev], inc=st[s], amt=16)
                k += 1
        P.add("sync", None, waits=[(c, c.n) for c in st if c.n])
        P.emit()


def phase_proj(nc, name, S, x_src, x_cast, wqk, items, rotary, wv, nhv, dv, qk_dst, v_dst, rope):
    ng = nhv * dv // 512
    NB = S // 512
    with nc.cleanup_on_exit():
        P = Phase(nc, name)
        xq = "gpsimd" if x_cast else "sync"
        xr = LoadRing(P, "x", 2, KC * 512, BF16, queue=xq)
        xr.schedule([x_src(tb) for tb in range(NB)])
        ws = LoadRing(P, "ws", 4, KC * 128, BF16)
        wsl = []
        for tb in range(NB):
            for (pt, st_, di, rb, sc) in items:
                wsl.append(wqk[pt * 128:(pt + 1) * 128, :])
                if st_ is not None:
                    wsl.append(wqk[st_ * 128:(st_ + 1) * 128, :])
        ws.schedule(wsl)
        wb = LoadRing(P, "wb", 2, KC * 512, BF16)
        wb.schedule([wv[g * 128:(g + 1) * 128, :] for tb in range(NB) for g in range(ng)])
        if rotary:
            rr = LoadRing(P, "rope", 2, 2 * 512, F32)
            rview = rope.ap().rearrange("(c p) t -> p c t", p=128)
            rr.schedule([rview[:, :, tb * 512:(tb + 1) * 512] for tb in range(NB)])
            tmp1 = P.sb("tmp1", [128, 512], F32)
            tmp2 = P.sb("tmp2", [128, 512], F32)
            c_tmp = P.ctr("tmp")
        pr = PsRing(P, "ps", 6)
        NQ = 4
        qst = P.sb("qst", [128, NQ * 512], BF16)
        c_qst = [P.ctr(f"qst{i}") for i in range(NQ)]
        vst = [P.sb(f"vst{i}", [128, nhv, 4, dv], BF16) for i in range(2)]
        c_vst = [P.ctr(f"vst{i}") for i in range(2)]
        c_pe = P.ctr("pe")
        c_ev = P.ctr("ev")
        c_ev2 = P.ctr("ev2")
        xr.start(); ws.start(); wb.start()
        if rotary:
            rr.start()
        wk = 0
        wbk = 0
        qk = 0
        nhg = 512 // dv
        vview = v_dst.ap().rearrange("(h p) c -> p h c", p=128)
        last_tmp = None
        for tb in range(NB):
            xo, xev = xr.get(tb)
            xb = xr.buf
            if rotary:
                ro, rev = rr.get(tb)
            last_pe = None
            for (pt, st_, di, rb, sc) in items:
                banks = []
                for which in ([0] if st_ is None else [0, 1]):
                    wo_, wev = ws.get(wk)
                    bi, bank, bfree = pr.get()
                    for kc in range(KC):
                        ev = P.add("tensor",
                                   mm(bank[:, :], ws.buf[:, wo_ + kc * 128:wo_ + (kc + 1) * 128],
                                      xb[:, xo + kc * 512:xo + (kc + 1) * 512], kc == 0, kc == KC - 1),
                                   waits=[wev, xev, bfree] if kc == 0 else [],
                                   inc=c_pe if kc == KC - 1 else None)
                    ws.done(wk, ev)
                    wk += 1
                    banks.append((bi, bank, ev))
                    last_pe = ev
                qs = qk % NQ
                qfree = (c_qst[qs], c_qst[qs].n) if c_qst[qs].n else None
                qo = qst[:, qs * 512:(qs + 1) * 512]
                if st_ is None:
                    bi, bank, pev = banks[0]
                    ev = P.add("scalar", act(qo, bank[:, :], AF.Copy, scale=float(sc)),
                               waits=[pev, qfree], inc=c_ev)
                    pr.done(bi, ev)
                else:
                    (bia, ba, pa), (bib, bb, pb) = banks
                    rb_ = rr.buf
                    e1 = P.add("vector", tt(tmp1[:, :], ba[:, :], rb_[:, ro:ro + 512], ALU.mult),
                               waits=[pa, rev, last_tmp], inc=c_tmp)
                    pr.done(bia, e1)
                    e2 = P.add("vector", tt(tmp2[:, :], bb[:, :], rb_[:, ro + 512:ro + 1024], ALU.mult),
                               waits=[pb, rev, last_tmp], inc=c_tmp)
                    pr.done(bib, e2)
                    ev = P.add("vector", tt(qo, tmp1[:, :], tmp2[:, :], ALU.add),
                               waits=[e1, e2, qfree], inc=c_tmp)
                    last_tmp = ev
                dst = qk_dst[di]
                P.add("sync", dma(dst[rb * 128:(rb + 1) * 128, tb * 512:(tb + 1) * 512], qo),
                      waits=[ev], inc=c_qst[qs], amt=16)
                qk += 1
            vs = tb % 2
            vfree = (c_vst[vs], c_vst[vs].n) if c_vst[vs].n else None
            vevs = []
            for g in range(ng):
                bo, bev = wb.get(wbk)
                for t4 in range(4):
                    bi, bank, bfree = pr.get()
                    for kc in range(KC):
                        ev = P.add("tensor",
                                   mm(bank[:, :], xb[:, xo + kc * 512 + t4 * 128:xo + kc * 512 + (t4 + 1) * 128],
                                      wb.buf[:, bo + kc * 512:bo + (kc + 1) * 512], kc == 0, kc == KC - 1),
                                   waits=[bev, xev, bfree] if kc == 0 else [],
                                   inc=c_pe if kc == KC - 1 else None)
                    last_pe = ev
                    outv = vst[vs][:, g * nhg:(g + 1) * nhg, t4, :]
                    inv = bank[:, :].rearrange("p (h d) -> p h d", d=dv)
                    if rotary:
                        e2 = P.add("scalar", act(outv, inv, AF.Copy), waits=[ev, vfree], inc=c_ev)
                    else:
                        e2 = P.add("vector", (lambda o, i: (lambda e: e.tensor_copy(o, i)))(outv, inv),
                                   waits=[ev, vfree], inc=c_ev2)
                    pr.done(bi, e2)
                    vevs.append(e2)
                wb.done(wbk, last_pe)
                wbk += 1
            P.add("sync", dma(vview[:, :, tb * 4 * dv:(tb + 1) * 4 * dv],
                              vst[vs][:, :, :, :].rearrange("p h t d -> p h (t d)")),
                  waits=[vevs[-1], vevs[3]], inc=c_vst[vs], amt=16)
            xr.done(tb, last_pe)
            if rotary:
                rr.done(tb, last_tmp)
        P.add("sync", None, waits=[(c, c.n) for c in c_qst + c_vst if c.n])
        P.emit()


def phase_att0(nc, S, qT, kT, v, oX, tri, msk, sel, NH=8, solo=False):
    NB = S // 512
    SO = S // 2
    with nc.cleanup_on_exit():
        P = Phase(nc, "a0")
        kb = [P.sb(f"k{i}", [128, S], BF16) for i in range(2)]
        qb = [P.sb(f"q{i}", [128, S], BF16) for i in range(2)]
        vb = [P.sb(f"v{i}", [128, S], BF16) for i in range(2)]
        c_ld = [P.ctr(f"ld{i}") for i in range(2)]
        negU = P.sb("negU", [128, 128], BF16)
        ones = P.sb("ones", [128, 128], BF16)
        mks = P.sb("mks", [128, 4 * 512], BF16)
        selt = P.sb("sel", [128, 2], F32)
        c_c = P.ctr("const")
        P.add("gpsimd", dma(negU[:, :], tri[:, :]), inc=c_c, amt=16)
        P.add("gpsimd", dma(mks[:, :], msk[0:128, :]), inc=c_c, amt=16)
        P.add("gpsimd", dma(selt[:, :], sel[:, :]), inc=c_c, amt=16)
        ev_ones = P.add("gpsimd", lambda e: e.memset(ones[:, :], 1.0), inc=c_c, amt=1)
        cev = (c_c, c_c.n)
        Z = [P.ps(f"z{i}") for i in range(3)]
        T = [P.ps(f"t{i}") for i in range(2)]
        O = [P.ps(f"o{i}") for i in range(2)]
        ebuf = [P.sb(f"e{i}", [128, 512], F32) for i in range(2)]
        spraw = [P.sb(f"spr{i}", [128, 512], BF16) for i in range(2)]
        spb = [P.sb(f"sp{i}", [128, 512], BF16) for i in range(3)]
        tmpb = [P.sb(f"tmp{i}", [128, 512], F32) for i in range(2)]
        araw = [P.sb(f"ar{i}", [128, 512], BF16) for i in range(2)]
        ab = [P.sb(f"a{i}", [128, 512], BF16) for i in range(3)]
        R = P.sb("R", [128, 512], F32)
        ost = [P.sb(f"ost{i}", [128, 512], BF16) for i in range(4)]
        c_ost = [P.ctr(f"ost{i}") for i in range(4)]
        c_z = P.ctr("z"); c_e = P.ctr("e"); c_spa = P.ctr("spa"); c_spp = P.ctr("spp")
        c_ut = P.ctr("ut"); c_tmp = P.ctr("tmp"); c_r = P.ctr("r")
        c_aa = P.ctr("aa"); c_ap = P.ctr("ap"); c_av = P.ctr("av"); c_oe = P.ctr("oe")

        tiles = []
        for h in range(NH):
            for qbi in range(NB):
                n = 4 * qbi + 4
                for s in range(n):
                    tiles.append(dict(h=h, qb=qbi, s=s, n=n, kt=n - 1 - s))
        G = len(tiles)
        st = [dict() for _ in range(G)]

        def load_head(h):
            sl = h % 2
            w = [hd_free[sl]]
            P.add("sync", dma(kb[sl][:, :], kT[h * 128:(h + 1) * 128, :]), waits=w, inc=c_ld[sl], amt=16)
            P.add("sync", dma(qb[sl][:, :], qT[h * 128:(h + 1) * 128, :]), waits=w, inc=c_ld[sl], amt=16)
            P.add("sync", dma(vb[sl][:, :], v[h * 128:(h + 1) * 128, :]), waits=w, inc=c_ld[sl], amt=16)
            hd_ld[h] = (c_ld[sl], c_ld[sl].n)

        hd_free = [None, None]
        hd_ld = {}
        load_head(0)
        if NH > 1:
            load_head(1)
        oX2 = oX
        ost_k = [0]

        def stage_z(g):
            t = tiles[g]
            sl = t["h"] % 2
            zf = st[g - 3]["tmp"] if g >= 3 else None
            st[g]["z"] = P.add("tensor", mm(Z[g % 3][:, :], kb[sl][:, t["kt"] * 128:(t["kt"] + 1) * 128],
                                            qb[sl][:, t["qb"] * 512:(t["qb"] + 1) * 512], True, False),
                               waits=[hd_ld[t["h"]], zf], inc=c_z)

        def stage_sp(g):
            t = tiles[g]
            diag = t["s"] < 4
            e_free = st[g - 2]["ln"] if g >= 2 else None
            ev = P.add("scalar", act(ebuf[g % 2][:, :], Z[g % 3][:, :], AF.Exp),
                       waits=[st[g]["z"], e_free], inc=c_e)
            sp_free = st[g - 3]["ut"] if g >= 3 else None
            if not diag:
                st[g]["ln"] = P.add("scalar", act(spb[g % 3][:, :], ebuf[g % 2][:, :], AF.Ln, bias=1.0),
                                    waits=[ev, sp_free], inc=c_spa)
                st[g]["sp"] = st[g]["ln"]
            else:
                kk = 3 - t["s"]
                rfree = st[g - 2].get("spm") if g >= 2 else None
                st[g]["ln"] = P.add("scalar", act(spraw[g % 2][:, :], ebuf[g % 2][:, :], AF.Ln, bias=1.0),
                                    waits=[ev, rfree], inc=c_spa)
                st[g]["spm"] = P.add("gpsimd", tt(spb[g % 3][:, :], spraw[g % 2][:, :],
                                                  mks[:, kk * 512:(kk + 1) * 512], ALU.mult),
                                     waits=[st[g]["ln"], sp_free, cev], inc=c_spp)
                st[g]["sp"] = st[g]["spm"]

        def stage_ut(g):
            t = tiles[g]
            tfree = st[g - 2].get("r") if g >= 2 else None
            P.add("tensor", mm(Z[g % 3][:, :], negU[:, :], spb[g % 3][:, :], False, True),
                  waits=[st[g]["sp"], cev])
            st[g]["ut"] = P.add("tensor", mm(T[g % 2][:, :], ones[:, :], spb[g % 3][:, :], True, True),
                                waits=[tfree], inc=c_ut)

        def stage_dve(g):
            t = tiles[g]
            if t["s"] == 0:
                prev = st[g - 1].get("tmp") if g >= 1 else None
                P.add("vector", lambda e: e.memset(R[:, :], 0.0), waits=[prev, st[g - 1].get("r") if g >= 1 else None],
                      inc=c_r)
                st[g]["r0"] = (c_r, c_r.n)
            rprev = st[g].get("r0") or st[g - 1].get("r")
            tfree = st[g - 2].get("a_act") if g >= 2 else None
            st[g]["tmp"] = P.add("vector", tt(tmpb[g % 2][:, :], Z[g % 3][:, :], R[:, :], ALU.subtract),
                                 waits=[st[g]["ut"], rprev, tfree], inc=c_tmp)
            if t["s"] < t["n"] - 1:
                st[g]["r"] = P.add("vector", tt(R[:, :], R[:, :], T[g % 2][:, :], ALU.add),
                                   waits=[st[g]["ut"], st[g]["tmp"]], inc=c_r)
            else:
                st[g]["r"] = st[g]["tmp"]

        def stage_a(g):
            t = tiles[g]
            diag = t["s"] < 4
            a_free = st[g - 3]["av"] if g >= 3 else None
            if not diag:
                st[g]["a_act"] = P.add("scalar", act(ab[g % 3][:, :], tmpb[g % 2][:, :], AF.Exp),
                                       waits=[st[g]["tmp"], a_free], inc=c_aa)
                st[g]["a"] = st[g]["a_act"]
            else:
                kk = 3 - t["s"]
                rfree = st[g - 2].get("am") if g >= 2 else None
                st[g]["a_act"] = P.add("scalar", act(araw[g % 2][:, :], tmpb[g % 2][:, :], AF.Exp),
                                       waits=[st[g]["tmp"], rfree], inc=c_aa)
                st[g]["am"] = P.add("gpsimd", tt(ab[g % 3][:, :], araw[g % 2][:, :],
                                                 mks[:, kk * 512:(kk + 1) * 512], ALU.mult),
                                    waits=[st[g]["a_act"], a_free], inc=c_ap)
                st[g]["a"] = st[g]["am"]

        o_free = [None, None]
        blk_idx = {}

        def stage_av(g):
            t = tiles[g]
            sl = t["h"] % 2
            bidx = t["h"] * NB + t["qb"]
            ob = bidx % 2
            st[g]["av"] = P.add("tensor", mm(O[ob][:, :], vb[sl][:, t["kt"] * 128:(t["kt"] + 1) * 128],
                                             ab[g % 3][:, :], t["s"] == 0, t["s"] == t["n"] - 1),
                                waits=[st[g]["a"], o_free[ob] if t["s"] == 0 else None], inc=c_av)
            if t["s"] == t["n"] - 1:
                evs = []
                for j in range(1 if solo else 2):
                    k = ost_k[0]
                    ost_k[0] += 1
                    os_ = k % 4
                    ofree = (c_ost[os_], c_ost[os_].n) if c_ost[os_].n else None
                    ev = P.add("scalar", act(ost[os_][:, :], O[ob][:, :], AF.Copy, scale=selt[:, j:j + 1]),
                               waits=[st[g]["av"], ofree, cev], inc=c_oe)
                    half = t["qb"] // (NB // 2)
                    lb = t["qb"] % (NB // 2)
                    row0 = half * 2 * 1024 + j * 1024 + t["h"] * 128
                    if solo:
                        row0 = t["h"] * 128
                        lb = t["qb"]
                    P.add("sync", dma(oX2[row0:row0 + 128, lb * 512:(lb + 1) * 512], ost[os_][:, :]),
                          waits=[ev], inc=c_ost[os_], amt=16)
                    evs.append(ev)
                o_free[ob] = evs[-1]
                if t["qb"] == NB - 1:
                    hd_free[sl] = st[g]["av"]
                    if t["h"] + 2 < NH:
                        load_head(t["h"] + 2)

        for tau in range(G + 2):
            if tau < G:
                stage_z(tau)
            if 0 <= tau - 1 < G:
                stage_ut(tau - 1)
            if 0 <= tau - 2 < G:
                stage_av(tau - 2)
            if tau < G:
                stage_sp(tau)
            if 0 <= tau - 1 < G:
                stage_dve(tau - 1)
                stage_a(tau - 1)
        P.add("sync", None, waits=[(c, c.n) for c in c_ost if c.n])
        P.emit()


def phase_att1(nc, S, qT, kT, v, oX, msk, sel, lam, sg, NH=4, solo=False):
    NB = S // 512
    with nc.cleanup_on_exit():
        P = Phase(nc, "a1")
        kb = [[P.sb(f"k{i}{c}", [128, S], BF16) for c in range(2)] for i in range(2)]
        vb = [P.sb(f"v{i}", [128, 2 * S], BF16) for i in range(2)]
        c_ld = [P.ctr(f"ld{i}") for i in range(2)]
        qblk = [P.sb(f"q{i}", [128, 2, 512], BF16) for i in range(2)]
        c_q = [P.ctr(f"q{i}") for i in range(2)]
        ones = P.sb("ones", [128, 128], F32)
        mkc = P.sb("mkc", [128, 4 * 512], BF16)
        selt = P.sb("sel", [128, 2], F32)
        lamt = P.sb("lam", [128, 4], F32)
        sgt = P.sb("sg", [128, 2], F32)
        gsel = P.sb("gsel", [128, 4], F32)
        lprod = P.sb("lprod", [128, 2], F32)
        lsum = P.sb("lsum", [128, 2], F32)
        neglam = P.sb("neglam", [128, 1], F32)
        c_c = P.ctr("const")
        P.add("gpsimd", dma(mkc[:, :], msk[128:256, :]), inc=c_c, amt=16)
        P.add("gpsimd", dma(selt[:, :], sel[:, :]), inc=c_c, amt=16)
        P.add("gpsimd", dma(lamt[:, :], lam[:, :]), inc=c_c, amt=16)
        P.add("gpsimd", dma(sgt[:, :], sg[:, :]), inc=c_c, amt=16)
        P.add("gpsimd", lambda e: e.memset(ones[:, :], 1.0), inc=c_c, amt=1)
        cev = (c_c, c_c.n)
        SB = [P.ps(f"s{i}") for i in range(3)]
        O = [[P.ps(f"o{c}{hf}") for hf in range(2)] for c in range(2)]
        MB = P.ps("misc")
        c_set = P.ctr("setup")
        e1 = P.add("vector", tt(lprod[:, 0:1], lamt[:, 0:1], lamt[:, 1:2], ALU.mult), waits=[cev], inc=c_set)
        e2 = P.add("vector", tt(lprod[:, 1:2], lamt[:, 2:3], lamt[:, 3:4], ALU.mult), waits=[cev], inc=c_set)
        e3 = P.add("tensor", mm(MB[:, 0:2], ones[:, :], lprod[:, :], True, True), waits=[e1, e2, cev], inc=c_set)
        e4 = P.add("scalar", act(lsum[:, :], MB[:, 0:2], AF.Exp), waits=[e3], inc=c_set)
        e5 = P.add("vector", tt(neglam[:, :], lsum[:, 1:2], lsum[:, 0:1], ALU.subtract), waits=[e4], inc=c_set)
        e6 = P.add("vector", lambda e: e.tensor_scalar(neglam[:, :], neglam[:, :], -float(LAMBDA_INIT), None, ALU.add),
                   waits=[e5], inc=c_set)
        for hf in range(2):
            for j in range(2):
                ev = P.add("vector", (lambda hf, j: lambda e: e.tensor_scalar(
                    gsel[:, hf * 2 + j:hf * 2 + j + 1], sgt[:, hf:hf + 1], selt[:, j:j + 1],
                    float(1.0 - LAMBDA_INIT), ALU.mult, ALU.mult))(hf, j), waits=[cev, e6], inc=c_set)
        setup_ev = ev
        misc_free = [e4]

        E = [P.sb(f"e{i}", [128, 512], BF16) for i in range(4)]
        eraw = [P.sb(f"er{i}", [128, 512], BF16) for i in range(2)]
        Dacc = [P.sb(f"d{c}", [128, 512], F32) for c in range(2)]
        rd = [P.sb(f"rd{c}", [128, 512], F32) for c in range(2)]
        t0 = P.sb("t0", [128, 512], F32)
        oh = [P.sb(f"oh{hf}", [128, 512], F32) for hf in range(2)]
        sq = [P.sb(f"sq{hf}", [128, 512], F32) for hf in range(2)]
        rstd = P.sb("rstd", [128, 512], F32)
        ost = [P.sb(f"ost{i}", [128, 512], BF16) for i in range(4)]
        c_ost = [P.ctr(f"ost{i}") for i in range(4)]
        c_s = P.ctr("s"); c_ea = P.ctr("ea"); c_ep = P.ctr("ep"); c_av = P.ctr("av")
        c_d = P.ctr("d"); c_f = P.ctr("fin"); c_fa = P.ctr("fina"); c_fp = P.ctr("finp")

        items = []
        for h in range(NH):
            for qbi in range(NB):
                n = 4 * qbi + 4
                for kt in range(n):
                    for c in range(2):
                        items.append(dict(h=h, qb=qbi, kt=kt, n=n, c=c, diag=(kt >= 4 * qbi), kk=kt - 4 * qbi))
        G = len(items)
        st = [dict() for _ in range(G)]
        hd_free = [None, None]
        hd_ld = {}
        q_free = [None, None]
        q_ld = {}
        vrow = v

        def load_head(h):
            sl = h % 2
            w = [hd_free[sl]]
            for c in range(2):
                P.add("sync", dma(kb[sl][c][:, :], kT[(h * 2 + c) * 128:(h * 2 + c + 1) * 128, :]),
                      waits=w, inc=c_ld[sl], amt=16)
            P.add("sync", dma(vb[sl][:, :], vrow[h * 128:(h + 1) * 128, :]), waits=w, inc=c_ld[sl], amt=16)
            hd_ld[h] = (c_ld[sl], c_ld[sl].n)

        def load_q(bidx):
            h, qbi = divmod(bidx, NB)
            sl = bidx % 2
            src = qT.ap().rearrange("(hc p) t -> p hc t", p=128)[:, h * 2:h * 2 + 2, qbi * 512:(qbi + 1) * 512]
            P.add("sync", dma(qblk[sl][:, :, :], src), waits=[q_free[sl]], inc=c_q[sl], amt=16)
            q_ld[bidx] = (c_q[sl], c_q[sl].n)

        load_head(0)
        if NH > 1:
            load_head(1)
        load_q(0)
        load_q(1)
        o_free = [None]
        ost_k = [0]
        d_last = [None, None]
        fin_read_e = [None]

        def stage_s(i):
            it = items[i]
            sl = it["h"] % 2
            bidx = it["h"] * NB + it["qb"]
            sf = st[i - 3]["e_act"] if i >= 3 else None
            st[i]["s"] = P.add("tensor", mm(SB[i % 3][:, :], kb[sl][it["c"]][:, it["kt"] * 128:(it["kt"] + 1) * 128],
                                            qblk[bidx % 2][:, it["c"], :], True, True),
                               waits=[hd_ld[it["h"]], q_ld[bidx], sf], inc=c_s)

        def stage_e(i):
            it = items[i]
            efree = [st[i - 4]["av"], st[i - 4]["d"]] if i >= 4 else []
            if not it["diag"]:
                st[i]["e_act"] = P.add("scalar", act(E[i % 4][:, :], SB[i % 3][:, :], AF.Exp, scale=float(SCALE)),
                                       waits=[st[i]["s"]] + efree, inc=c_ea)
                st[i]["e"] = st[i]["e_act"]
            else:
                rfree = st[i - 2].get("em") if i >= 2 else None
                st[i]["e_act"] = P.add("scalar", act(eraw[i % 2][:, :], SB[i % 3][:, :], AF.Exp, scale=float(SCALE)),
                                       waits=[st[i]["s"], rfree], inc=c_ea)
                st[i]["em"] = P.add("gpsimd", tt(E[i % 4][:, :], eraw[i % 2][:, :],
                                                 mkc[:, it["kk"] * 512:(it["kk"] + 1) * 512], ALU.mult),
                                    waits=[st[i]["e_act"], cev] + efree, inc=c_ep)
                st[i]["e"] = st[i]["em"]

        def stage_av(i):
            it = items[i]
            sl = it["h"] % 2
            c = it["c"]
            first = it["kt"] == 0
            last = it["kt"] == it["n"] - 1
            for hf in range(2):
                ev = P.add("tensor", mm(O[c][hf][:, :],
                                        vb[sl][:, it["kt"] * 256 + hf * 128:it["kt"] * 256 + (hf + 1) * 128],
                                        E[i % 4][:, :], first, last),
                           waits=[st[i]["e"], o_free[0] if first else None],
                           inc=c_av if hf == 1 else None)
            st[i]["av"] = ev
            if first:
                st[i]["d"] = P.add("vector", (lambda o, a: lambda e: e.tensor_copy(o, a))(Dacc[c][:, :], E[i % 4][:, :]),
                                   waits=[st[i]["e"], fin_read_e[0], d_last[c]], inc=c_d)
            else:
                st[i]["d"] = P.add("vector", tt(Dacc[c][:, :], Dacc[c][:, :], E[i % 4][:, :], ALU.add),
                                   waits=[st[i]["e"], d_last[c]], inc=c_d)
            d_last[c] = st[i]["d"]
            if last and c == 1:
                finalize(i)

        def finalize(i):
            it = items[i]
            sl = it["h"] % 2
            bidx = it["h"] * NB + it["qb"]
            av0 = st[i - 1]["av"]
            av1 = st[i]["av"]
            q_free[bidx % 2] = st[i]["s"]
            if bidx + 2 < NH * NB:
                load_q(bidx + 2)
            if it["qb"] == NB - 1:
                hd_free[sl] = av1
                if it["h"] + 2 < NH:
                    load_head(it["h"] + 2)
            prev = misc_free[0]
            last_o_read = None
            for c in range(2):
                ev = P.add("tensor", mm(MB[:, :], ones[:, :], Dacc[c][:, :], True, True),
                           waits=[d_last[c], prev, cev], inc=c_f)
                prev = P.add("vector", (lambda o, a: lambda e: e.reciprocal(o, a))(rd[c][:, :], MB[:, :]),
                             waits=[ev, fin_read_e[0]], inc=c_f)
            fin_read_e[0] = prev
            for hf in range(2):
                e0 = P.add("vector", tt(t0[:, :], O[0][hf][:, :], rd[0][:, :], ALU.mult),
                           waits=[av0, av1, prev, fin_prev[0]], inc=c_f)
                e1_ = P.add("vector", tt(oh[hf][:, :], O[1][hf][:, :], rd[1][:, :], ALU.mult),
                            waits=[av1, prev, e0], inc=c_f)
                prev = P.add("vector", (lambda hf: lambda e: e.scalar_tensor_tensor(
                    oh[hf][:, :], oh[hf][:, :], neglam[:, 0:1], t0[:, :], ALU.mult, ALU.add))(hf),
                    waits=[e0, e1_, setup_ev], inc=c_f)
                last_o_read = e1_
                evs = P.add("gpsimd", tt(sq[hf][:, :], oh[hf][:, :], oh[hf][:, :], ALU.mult),
                            waits=[prev, ss_done[0]], inc=c_fp)
                st[i][f"sq{hf}"] = evs
                st[i][f"oh{hf}"] = prev
            o_free[0] = last_o_read
            ev = P.add("tensor", mm(MB[:, :], ones[:, :], sq[0][:, :], True, False),
                       waits=[st[i]["sq0"], prev])
            ev = P.add("tensor", mm(MB[:, :], ones[:, :], sq[1][:, :], False, True),
                       waits=[st[i]["sq1"]], inc=c_f)
            ss_done[0] = ev
            ev = P.add("scalar", act(rstd[:, :], MB[:, :], AF.Ln, scale=1.0 / 256.0, bias=float(SUBLN_EPS)),
                       waits=[ev], inc=c_fa)
            misc_free[0] = ev
            ev = P.add("scalar", act(rstd[:, :], rstd[:, :], AF.Exp, scale=-0.5), waits=[ev], inc=c_fa)
            half = it["qb"] // (NB // 2)
            lb = it["qb"] % (NB // 2)
            for hf in range(2):
                evn = P.add("vector", tt(oh[hf][:, :], oh[hf][:, :], rstd[:, :], ALU.mult),
                            waits=[ev, st[i][f"sq{hf}"]], inc=c_f)
                for j in range(1 if solo else 2):
                    k = ost_k[0]
                    ost_k[0] += 1
                    os_ = k % 4
                    ofree = (c_ost[os_], c_ost[os_].n) if c_ost[os_].n else None
                    e2_ = P.add("scalar", act(ost[os_][:, :], oh[hf][:, :], AF.Copy,
                                              scale=gsel[:, hf * 2 + j:hf * 2 + j + 1]),
                                waits=[evn, ofree, setup_ev], inc=c_fa)
                    row0 = half * 2 * 1024 + j * 1024 + (it["h"] * 2 + hf) * 128
                    if solo:
                        row0 = (it["h"] * 2 + hf) * 128
                        lb = it["qb"]
                    P.add("sync", dma(oX[row0:row0 + 128, lb * 512:(lb + 1) * 512], ost[os_][:, :]),
                          waits=[e2_], inc=c_ost[os_], amt=16)
                st[i][f"ohfree{hf}"] = e2_
            fin_prev[0] = e2_

        fin_prev = [None]
        ss_done = [None]
        for tau in range(G + 1):
            if tau < G:
                stage_s(tau)
            if 0 <= tau - 1 < G:
                stage_av(tau - 1)
            if tau < G:
                stage_e(tau)
        P.add("sync", None, waits=[(c, c.n) for c in c_ost if c.n])
        P.emit()


def phase_post(nc, name, S, oT, xres_src, wo, wup, wdn, lnp, ln_base, dst32, dst16, ntok=None):
    SO = ntok if ntok is not None else S // 2
    NBO = SO // 512
    with nc.cleanup_on_exit():
        P = Phase(nc, name)
        xres = P.sb("xres", [128, KC, 512], F32)
        ob = P.sb("ob", [128, KC, 512], BF16)
        xbf = P.sb("xbf", [128, KC, 512], BF16)
        H = P.sb("H", [128, HC, 512], BF16)
        lnt = P.sb("lnp", [128, 128], F32)
        ones = P.sb("ones", [128, 128], F32)
        acc1 = P.sb("acc1", [128, 512], F32)
        acc2 = P.sb("acc2", [128, 512], F32)
        sqt = [P.sb(f"sqt{i}", [128, 512], F32) for i in range(2)]
        mean = P.sb("mean", [128, 512], F32)
        msq = P.sb("msq", [128, 512], F32)
        var = P.sb("var", [128, 512], F32)
        rstd = P.sb("rstd", [128, 512], F32)
        ua = P.sb("ua", [128, 512], F32)
        ub = P.sb("ub", [128, 512], F32)
        rt = [P.sb(f"rt{i}", [128, 512], F32) for i in range(3)]
        ws = LoadRing(P, "ws", 4, KC * 128, BF16)
        wb = LoadRing(P, "wb", 2, HC * 128, BF16)
        wsl = []
        wbl = []
        for tb in range(NBO):
            wsl += [wo[n * 128:(n + 1) * 128, :] for n in range(KC)]
            wsl += [wup[n * 128:(n + 1) * 128, :] for n in range(HC)]
            wbl += [wdn[n * 128:(n + 1) * 128, :] for n in range(KC)]
        ws.schedule(wsl)
        wb.schedule(wbl)
        pr = PsRing(P, "ps", 6)
        M1 = P.ps("m1")
        M2 = P.ps("m2")
        c_c = P.ctr("const")
        P.add("sync", dma(lnt[:, :], lnp[:, :]), inc=c_c, amt=16)
        P.add("gpsimd", lambda e: e.memset(ones[:, :], 1.0), inc=c_c, amt=1)
        cev = (c_c, c_c.n)
        c_lo = P.ctr("ldo"); c_lx = P.ctr("ldx")
        c_pe = P.ctr("pe"); c_v = P.ctr("dve"); c_a = P.ctr("act"); c_g = P.ctr("pool")
        c_s32 = P.ctr("st32"); c_s16 = P.ctr("st16")
        ws.start(); wb.start()
        oview = oT.ap().rearrange("(kc p) t -> p kc t", p=128)
        xview = xres_src.ap().rearrange("(kc p) t -> p kc t", p=128)
        d32v = dst32.ap().rearrange("(kc p) t -> p kc t", p=128)
        d16v = dst16.ap().rearrange("(kc p) t -> p kc t", p=128) if dst16 is not None else None
        wk = 0
        wbk = 0
        ob_free = None
        xres_free = None
        xbf_free = None
        H_free = None
        m_free = [None]
        ln_scratch_free = [None]

        def layer_norm(l_idx, ready, write_bf_waits):
            gcol = (ln_base + l_idx) * 2 * KC
            ev1 = P.add("vector", tt(acc1[:, :], xres[:, 0, :], xres[:, 1, :], ALU.add),
                        waits=[ready, ln_scratch_free[0]], inc=c_v)
            for kc in range(2, KC):
                ev1 = P.add("vector", tt(acc1[:, :], acc1[:, :], xres[:, kc, :], ALU.add), waits=[ev1], inc=c_v)
            ev2 = None
            sq_free = [None, None]
            for kc in range(KC):
                es = P.add("scalar", act(sqt[kc % 2][:, :], xres[:, kc, :], AF.Square),
                           waits=[ready, sq_free[kc % 2]], inc=c_a)
                if kc == 0:
                    ev2 = P.add("gpsimd", (lambda o, a: lambda e: e.tensor_copy(o, a))(acc2[:, :], sqt[0][:, :]),
                                waits=[es, ln_scratch_free[0]], inc=c_g)
                else:
                    ev2 = P.add("gpsimd", tt(acc2[:, :], acc2[:, :], sqt[kc % 2][:, :], ALU.add),
                                waits=[es, ev2], inc=c_g)
                sq_free[kc % 2] = ev2
            p1 = P.add("tensor", mm(M1[:, :], ones[:, :], acc1[:, :], True, True), waits=[ev1, m_free[0], cev], inc=c_pe)
            p2 = P.add("tensor", mm(M2[:, :], ones[:, :], acc2[:, :], True, True), waits=[ev2], inc=c_pe)
            e = P.add("vector", lambda e: e.tensor_scalar(mean[:, :], M1[:, :], 1.0 / D, None, ALU.mult),
                      waits=[p1], inc=c_v)
            e = P.add("vector", tt(msq[:, :], mean[:, :], mean[:, :], ALU.mult), waits=[e], inc=c_v)
            e = P.add("vector", lambda e: e.scalar_tensor_tensor(var[:, :], M2[:, :], 1.0 / D, msq[:, :],
                                                                 ALU.mult, ALU.subtract),
                      waits=[p2, e], inc=c_v)
            m_free[0] = e
            e = P.add("scalar", act(rstd[:, :], var[:, :], AF.Ln, bias=float(LN_EPS)), waits=[e], inc=c_a)
            e = P.add("scalar", act(rstd[:, :], rstd[:, :], AF.Exp, scale=-0.5), waits=[e], inc=c_a)
            rs_ev = e
            outs = []
            lastv = None
            lastg = None
            for kc in range(KC):
                if kc % 2 == 0:
                    eng, cc, u, prev = "vector", c_v, ua, lastv
                else:
                    eng, cc, u, prev = "gpsimd", c_g, ub, lastg
                e = P.add(eng, tt(u[:, :], xres[:, kc, :], mean[:, :], ALU.subtract), waits=[rs_ev, prev], inc=cc)
                e = P.add(eng, tt(u[:, :], u[:, :], rstd[:, :], ALU.mult), waits=[e], inc=cc)
                e = P.add(eng, (lambda kc, u: lambda e_: e_.tensor_scalar(
                    xres[:, kc, :], u[:, :], lnt[:, gcol + kc:gcol + kc + 1],
                    lnt[:, gcol + KC + kc:gcol + KC + kc + 1], ALU.mult, ALU.add))(kc, u),
                    waits=[e, cev], inc=cc)
                if kc % 2 == 0:
                    lastv = e
                else:
                    lastg = e
                eb = P.add("scalar", act(xbf[:, kc, :], xres[:, kc, :], AF.Copy),
                           waits=[e] + (write_bf_waits if kc == 0 else []), inc=c_a)
                outs.append(eb)
            ln_scratch_free[0] = outs[-1]
            return outs[-1], lastv, lastg

        for tb in range(NBO):
            eo = P.add("sync", dma(ob[:, :, :], oview[:, :, tb * 512:(tb + 1) * 512]), waits=[ob_free], inc=c_lo, amt=16)
            ex = P.add("sync", dma(xres[:, :, :], xview[:, :, tb * 512:(tb + 1) * 512]), waits=[xres_free],
                       inc=c_lx, amt=16)
            last = None
            for n in range(KC):
                wo_, wev = ws.get(wk)
                bi, bank, bfree = pr.get()
                for kc in range(KC):
                    ev = P.add("tensor", mm(bank[:, :], ws.buf[:, wo_ + kc * 128:wo_ + (kc + 1) * 128],
                                            ob[:, kc, :], kc == 0, kc == KC - 1),
                               waits=[wev, eo, bfree] if kc == 0 else [], inc=c_pe if kc == KC - 1 else None)
                ws.done(wk, ev)
                wk += 1
                last_pe = ev
                e = P.add("vector", (lambda n, bank: lambda e_: e_.scalar_tensor_tensor(
                    xres[:, n, :], xres[:, n, :], float(ALPHA), bank[:, :], ALU.mult, ALU.add))(n, bank),
                    waits=[ev, ex], inc=c_v)
                pr.done(bi, e)
                last = e
            ob_free = last_pe
            ln1_ev, lv, lg = layer_norm(0, last, [xbf_free])
            hk = 0
            for hc in range(HC):
                wo_, wev = ws.get(wk)
                bi, bank, bfree = pr.get()
                for kc in range(KC):
                    ev = P.add("tensor", mm(bank[:, :], ws.buf[:, wo_ + kc * 128:wo_ + (kc + 1) * 128],
                                            xbf[:, kc, :], kc == 0, kc == KC - 1),
                               waits=[wev, ln1_ev, bfree] if kc == 0 else [], inc=c_pe if kc == KC - 1 else None)
                ws.done(wk, ev)
                wk += 1
                up_last = ev
                r = rt[hc % 3]
                e = P.add("scalar", act(r[:, :], bank[:, :], AF.Relu), waits=[ev, rt_free[hc % 3]], inc=c_a)
                pr.done(bi, e)
                if hc % 4 == 3:
                    e2 = P.add("gpsimd", tt(H[:, hc, :], r[:, :], r[:, :], ALU.mult),
                               waits=[e, H_free if hc < 8 else None], inc=c_g)
                else:
                    e2 = P.add("vector", tt(H[:, hc, :], r[:, :], r[:, :], ALU.mult),
                               waits=[e, H_free if hc < 8 else None], inc=c_v)
                rt_free[hc % 3] = e2
                h_evs[hc] = e2
            last = None
            for n in range(KC):
                bo, bev = wb.get(wbk)
                bi, bank, bfree = pr.get()
                for hc in range(HC):
                    ev = P.add("tensor", mm(bank[:, :], wb.buf[:, bo + hc * 128:bo + (hc + 1) * 128],
                                            H[:, hc, :], hc == 0, hc == HC - 1),
                               waits=([bev, bfree] if hc == 0 else []) + ([h_evs[hc]] if n == 0 else []),
                               inc=c_pe if hc == HC - 1 else None)
                wb.done(wbk, ev)
                wbk += 1
                e = P.add("vector", (lambda n, bank: lambda e_: e_.scalar_tensor_tensor(
                    xres[:, n, :], xres[:, n, :], float(ALPHA), bank[:, :], ALU.mult, ALU.add))(n, bank),
                    waits=[ev, lv, lg], inc=c_v)
                pr.done(bi, e)
                last = e
            H_free = ev
            ln2_ev, lv2, lg2 = layer_norm(1, last, [up_last])
            s32 = P.add("sync", dma(d32v[:, :, tb * 512:(tb + 1) * 512], xres[:, :, :]),
                        waits=[lv2, lg2], inc=c_s32, amt=16)
            xres_free = s32
            if d16v is not None:
                s16 = P.add("sync", dma(d16v[:, :, tb * 512:(tb + 1) * 512], xbf[:, :, :]),
                            waits=[ln2_ev], inc=c_s16, amt=16)
                xbf_free = s16
            else:
                xbf_free = ln2_ev
        P.add("sync", None, waits=[(c, c.n) for c in (c_s32, c_s16) if c.n])
        P.emit()


rt_free = [None, None, None]
h_evs = {}


def phase_cc(nc, name, ops):
    with nc.cleanup_on_exit():
        P = Phase(nc, name)
        c = P.ctr("cc")
        for (kind, op, groups, src, dst) in ops:
            P.add("gpsimd", (lambda kind, op, groups, src, dst: lambda e: e.collective_compute(
                kind, op, replica_groups=groups, ins=[src.ap().opt()], outs=[dst.ap().opt()]))(kind, op, groups, src, dst),
                inc=c, amt=None)
        P.add("gpsimd", None, waits=[(c, c.n)])
        P.emit()


WSPEC = [
    ("wqk0", 32, 2048), ("wv0", 4, 8192), ("wo0", 16, 2048), ("wup0", 64, 2048), ("wdn0", 16, 8192),
    ("wqk1", 64, 2048), ("wv1", 4, 8192), ("wo1", 16, 2048), ("wup1", 64, 2048), ("wdn1", 16, 8192),
]
NCORES = 4


def build_program(S, stop_after=99, dbg=()):
    global rt_free, h_evs
    NT = S // 128
    nc = bass.Bass("TRN2", target_bir_lowering=False)

    def din(name, shape):
        return nc.dram_tensor(name, shape, F32, kind="ExternalInput")

    def dsc(name, shape, dt):
        if name in dbg:
            return nc.dram_tensor(name, shape, dt, kind="ExternalOutput")
        return nc.dram_tensor(name, shape, dt)

    xT = din("xT", [D, S])
    w32 = {n: din(n, [nt * 128, W]) for (n, nt, W) in WSPEC}
    lnp = din("lnp", [128, 128])
    lam = din("lam", [128, 4])
    sg = din("sg", [128, 2])
    sel = din("sel", [128, 2])
    rope = din("rope", [256, S])
    tri = din("tri", [128, 128])
    msk = din("msk", [256, 2048])
    out = nc.dram_tensor("out", [D, S], F32, kind="ExternalOutput")

    w16 = {n: nc.dram_tensor(n + "_b", [nt * 128, W], BF16) for (n, nt, W) in WSPEC}
    x0b = nc.dram_tensor("x0b", [D, S], BF16)
    qT0 = dsc("qT0", [16 * 128, S], BF16)
    kT0 = dsc("kT0", [16 * 128, S], BF16)
    v0 = dsc("v0", [16 * 128, NT * 128], BF16)
    oT0 = dsc("oT0", [D, S], BF16)
    x1res = dsc("x1res", [D, S], F32)
    x1b = nc.dram_tensor("x1b", [D, S], BF16)
    qT1 = dsc("qT1", [16 * 128, S], BF16)
    kT1 = dsc("kT1", [16 * 128, S], BF16)
    v1 = dsc("v1", [8 * 128, NT * 256], BF16)
    oT1 = dsc("oT1", [D, S], BF16)

    ph = 0

    def go():
        nonlocal ph
        ph += 1
        return ph <= stop_after

    def xsrc(xb):
        xa = xb.ap().rearrange("(kc p) t -> p kc t", p=128)
        return lambda tb: xa[:, :, tb * 512:(tb + 1) * 512]

    if go():
        phase_prep(nc, [(w32[n], w16[n], nt * 128, W) for (n, nt, W) in WSPEC] + [(xT, x0b, D, S)])
    if go():
        items0 = [(i, None, 0 if i < 16 else 1, i % 16, SCALE if i < 16 else 1.0) for i in range(32)]
        phase_proj(nc, "p0", S, xsrc(x0b), False, w16["wqk0"], items0, False,
                   w16["wv0"], 16, 128, [qT0, kT0], v0, None)
    if go():
        phase_att0(nc, S, qT0, kT0, v0, oT0, tri, msk, sel, NH=16, solo=True)
    if go():
        rt_free = [None, None, None]; h_evs = {}
        phase_post(nc, "po0", S, oT0, xT, w16["wo0"], w16["wup0"], w16["wdn0"], lnp, 0, x1res, x1b, ntok=S)
    if go():
        items1 = [(i if i < 16 else 32 + (i - 16), 16 + i if i < 16 else 48 + (i - 16), 0 if i < 16 else 1, i % 16, 1.0)
                  for i in range(32)]
        phase_proj(nc, "p1", S, xsrc(x1b), False,
                   w16["wqk1"], items1, True, w16["wv1"], 8, 256, [qT1, kT1], v1, rope)
    if go():
        phase_att1(nc, S, qT1, kT1, v1, oT1, msk, sel, lam, sg, NH=8, solo=True)
    if go():
        rt_free = [None, None, None]; h_evs = {}
        phase_post(nc, "po1", S, oT1, x1res, w16["wo1"], w16["wup1"], w16["wdn1"], lnp, 2, out, None, ntok=S)
    return nc


def tile_w(W, cw):
    K, N = W.shape
    kc = K // 128
    nt = N // cw
    return np.ascontiguousarray(W.reshape(kc, 128, nt, cw).transpose(2, 1, 0, 3).reshape(nt * 128, kc * cw))


def rope_tables(S):
    inv_freq = (np.float32(500000.0) ** (-np.arange(0, 32, 2, dtype=np.float32) / np.float32(32))).astype(np.float32)
    ang = (np.arange(S, dtype=np.float32)[:, None] * inv_freq[None, :]).astype(np.float32)
    cos = np.cos(ang).astype(np.float32).T
    sin = np.sin(ang).astype(np.float32).T
    C = np.ones((128, S), np.float32)
    Sn = np.zeros((128, S), np.float32)
    C[0:16] = cos
    C[16:32] = cos
    Sn[0:16] = -sin
    Sn[16:32] = sin
    return np.concatenate([C, Sn], 0)


def const_tables():
    i = np.arange(128)
    tri = -(i[:, None] >= i[None, :]).astype(np.float32)
    p = np.arange(128)[:, None, None]
    kk = np.arange(4)[None, :, None]
    t = np.arange(512)[None, None, :]
    key = kk * 128 + p
    strict = (key < t).astype(np.float32).reshape(128, 2048)
    causal = (key <= t).astype(np.float32).reshape(128, 2048)
    return tri, np.concatenate([strict, causal], 0)


def swap_cols(Wc):
    o = Wc.copy()
    o[:, 0:16] = Wc[:, 16:32]
    o[:, 16:32] = Wc[:, 0:16]
    return o


def make_in_maps(S, x, ln_g, ln_b, sb_w_qkv, sb_w_o, kv_w, diff_w_q, diff_lambda, diff_subln_g, diff_w_o,
                 mlp_w_up, mlp_w_down):
    B = x.shape[0]
    SO = S // 2
    tri, msk = const_tables()
    rope = rope_tables(S)
    lnp = np.zeros((128, 128), np.float32)
    for l in range(2):
        for i in range(2):
            base = (l * 2 + i) * 2 * KC
            lnp[:, base:base + KC] = ln_g[l, i].reshape(KC, 128).T
            lnp[:, base + KC:base + 2 * KC] = ln_b[l, i].reshape(KC, 128).T
    lam = np.ascontiguousarray(diff_lambda[0].T)
    sg = np.ascontiguousarray(diff_subln_g[0].reshape(2, 128).T)
    Wqkv = sb_w_qkv[0]
    q1 = diff_w_q[0]
    k1 = kv_w[:, 0:2048]
    v1 = kv_w[:, 2048:4096]
    q1s = np.concatenate([swap_cols(q1[:, c * 128:(c + 1) * 128]) for c in range(16)], 1)
    k1s = np.concatenate([swap_cols(k1[:, c * 128:(c + 1) * 128]) for c in range(16)], 1)
    shared = {
        "wqk0": tile_w(Wqkv[:, 0:4096], 128), "wv0": tile_w(Wqkv[:, 4096:6144], 512),
        "wo0": tile_w(sb_w_o[0], 128), "wup0": tile_w(mlp_w_up[0], 128), "wdn0": tile_w(mlp_w_down[0], 128),
        "wqk1": tile_w(np.concatenate([q1, q1s, k1, k1s], 1), 128), "wv1": tile_w(v1, 512),
        "wo1": tile_w(diff_w_o[0], 128), "wup1": tile_w(mlp_w_up[1], 128), "wdn1": tile_w(mlp_w_down[1], 128),
    }
    selv = np.ones((128, 2), np.float32)
    in_maps = []
    for b in range(B):
        m = dict(xT=np.ascontiguousarray(x[b].T), lnp=lnp, lam=lam, sg=sg, rope=rope, tri=tri, msk=msk, sel=selv)
        m.update(shared)
        in_maps.append(m)
    return in_maps


def kernel(x, ln_g, ln_b, sb_w_qkv, sb_w_o, kv_w, diff_w_q, diff_lambda, diff_subln_g, diff_w_o,
           mlp_w_up, mlp_w_down):
    args = [np.asarray(a, dtype=np.float32) for a in (x, ln_g, ln_b, sb_w_qkv, sb_w_o, kv_w, diff_w_q, diff_lambda,
                                                      diff_subln_g, diff_w_o, mlp_w_up, mlp_w_down)]
    B, S, _ = args[0].shape
    assert B == NCORES
    in_maps = make_in_maps(S, *args)
    nc = build_program(S)
    res = run_bass_kernel_spmd(nc, in_maps, core_ids=list(range(NCORES)))
    outp = np.empty((B, S, D), np.float32)
    for b in range(B):
        outp[b] = res.results[b]["out"].T
    return outp
```
